# Optimizing a Trainium2 kernel written in Bass

```python
import math
import jax
import jax.numpy as jnp
from jax import lax
import numpy as np

D_MODEL = 2048
BATCH = 1
SEQ = 8192
DEPTH = 1

GRID_W = 64
CTX_LEN = 256
N_MOD = 9
D_FF = 5632
NORM_EPS = 1e-6

MLA_HEADS = 16
Q_LORA = 512
KV_LORA = 512
QK_NOPE = 128
QK_ROPE = 64
V_HEAD = 128
ROPE_AXIS = QK_ROPE // 2
ROPE_THETA = 10000.0
ATTN_SCALE = (QK_NOPE + QK_ROPE) ** -0.5
Q_BLOCK = 128

HY_WIDTH = 1024
HY_ORDER = 2
HY_SHORT = 3
HY_EMB = 33
HY_BANDS = (HY_EMB - 1) // 2
HY_FILTER_W = 64
HY_TARGET = 1e-2
HY_FAST_DECAY = 0.3
HY_SLOW_DECAY = 1.5
HY_MAX_DECAY = math.log(HY_TARGET) / HY_FAST_DECAY
HY_MIN_DECAY = math.log(HY_TARGET) / HY_SLOW_DECAY
HY_MOD_SHIFT = 0.05

Q_END = Q_LORA
KV_END = Q_END + KV_LORA
KR_END = KV_END + QK_ROPE
HY_END = KR_END + 3 * HY_WIDTH
D_IN = HY_END + 2 * D_MODEL

kernel_name = 'hybrid_mla_hyena_dit_block'


def rms_norm(x, g):
    xf = x.astype(jnp.float32)
    y = xf * lax.rsqrt(jnp.mean(xf * xf, axis=-1, keepdims=True) + NORM_EPS)
    return (y * g.astype(jnp.float32)).astype(x.dtype)


def modulate(h, shift, scale):
    return h * (1.0 + scale) + shift


def swiglu(h, w13, w2):
    a, b = jnp.split(h @ w13, 2, axis=-1)
    return (jax.nn.silu(a) * b) @ w2


def half_ffn(s, shift, scale, gate, g, w13, w2):
    return s + 0.5 * gate * swiglu(modulate(rms_norm(s, g), shift, scale), w13, w2)


def split_proj(p):
    return p[..., :Q_END], p[..., Q_END:KV_END], p[..., KV_END:KR_END], p[..., KR_END:HY_END], p[..., HY_END:]


def rope_half(x, ang):
    x1, x2 = jnp.split(x, 2, axis=-1)
    cos = jnp.cos(ang).astype(x.dtype)
    sin = jnp.sin(ang).astype(x.dtype)
    return jnp.concatenate([x1 * cos - x2 * sin, x1 * sin + x2 * cos], axis=-1)


def rope_2d(x, ang_row, ang_col):
    xr, xc = jnp.split(x, 2, axis=-1)
    return jnp.concatenate([rope_half(xr, ang_row), rope_half(xc, ang_col)], axis=-1)


def mla_query(p_q, g_q, w_uq):
    B, L, _ = p_q.shape
    q = (rms_norm(p_q, g_q) @ w_uq).reshape(B, L, MLA_HEADS, QK_NOPE + QK_ROPE)
    return q[..., :QK_NOPE], q[..., QK_NOPE:]


def mla_kv(p_kv, g_kv, w_ukv):
    B, L, _ = p_kv.shape
    kv = (rms_norm(p_kv, g_kv) @ w_ukv).reshape(B, L, MLA_HEADS, QK_NOPE + V_HEAD)
    return kv[..., :QK_NOPE], kv[..., QK_NOPE:]


def join_key(k_nope, k_rope):
    kr = jnp.broadcast_to(k_rope[:, :, None, :], k_nope.shape[:-1] + (QK_ROPE,))
    return jnp.concatenate([k_nope, kr], axis=-1)


def softmax_attend(q, k, v):
    s = jnp.einsum('bqhd,bkhd->bhqk', q, k, preferred_element_type=jnp.float32) * ATTN_SCALE
    p = jax.nn.softmax(s, axis=-1).astype(v.dtype)
    return jnp.einsum('bhqk,bkhd->bqhd', p, v)


def blocked_attention(q, k, v):
    B, S, H, Dq = q.shape
    nb = S // Q_BLOCK
    qb = q.reshape(B, nb, Q_BLOCK, H, Dq).transpose(1, 0, 2, 3, 4)
    out = lax.map(lambda qblk: softmax_attend(qblk, k, v), qb)
    return out.transpose(1, 0, 2, 3, 4).reshape(B, S, H, v.shape[-1])


def hyena_filters(L, w1, b1, w2, b2, w3, freq):
    f32 = lambda a: a.astype(jnp.float32)
    pos = jnp.arange(L, dtype=jnp.float32)[:, None]
    t01 = pos / max(L - 1, 1)
    bands = jnp.linspace(1e-4, HY_BANDS - 1, HY_BANDS, dtype=jnp.float32)[None, :]
    ang = bands * (2.0 * math.pi / L) * pos
    z = jnp.concatenate([t01, jnp.cos(ang), -jnp.sin(ang)], axis=-1)
    h = jnp.sin(f32(freq) * (z @ f32(w1) + f32(b1)))
    h = jnp.sin(f32(freq) * (h @ f32(w2) + f32(b2)))
    h = (h @ f32(w3)).reshape(L, HY_ORDER, HY_WIDTH)
    dist = (jnp.abs(pos - (L // 2)) / (L / 2.0))[:, :, None]
    decay = jnp.abs(jnp.linspace(HY_MIN_DECAY, HY_MAX_DECAY, HY_WIDTH, dtype=jnp.float32))
    h = h * (jnp.exp(-dist * decay) + HY_MOD_SHIFT)
    return h * lax.rsqrt(jnp.sum(h * h, axis=0, keepdims=True) + NORM_EPS)


def short_conv(u, w, b):
    L = u.shape[1]
    up = jnp.pad(u, ((0, 0), (1, 1), (0, 0)))
    return up[:, :L] * w[0] + up[:, 1:L + 1] * w[1] + up[:, 2:] * w[2] + b


def fft_long_conv(u, h):
    L = u.shape[1]
    n = 2 * L
    uf = jnp.fft.rfft(u.astype(jnp.float32), n=n, axis=1)
    hf = jnp.fft.rfft(h, n=n, axis=0)
    y = jnp.fft.irfft(uf * hf[None], n=n, axis=1)[:, L // 2:L // 2 + L]
    return y.astype(u.dtype)


def hyena(p_hy, conv_w, conv_b, filt, skip):
    v, x1, x2 = jnp.split(short_conv(p_hy, conv_w, conv_b), 3, axis=-1)
    z = x1 * (fft_long_conv(v, filt[:, 0]) + skip[0] * v)
    return x2 * (fft_long_conv(z, filt[:, 1]) + skip[1] * z)


def merge_branches(y_attn, y_hy, p_gate, w_attn_o, w_hy_o, w_out):
    g_attn, g_hy = jnp.split(p_gate, 2, axis=-1)
    y = jax.nn.sigmoid(g_attn) * (y_attn @ w_attn_o) + jax.nn.sigmoid(g_hy) * (y_hy @ w_hy_o)
    return y @ w_out


def setup_inputs(seed: int = 0) -> dict:
    key = jax.random.key(seed)
    ks = iter(jax.random.split(key, 40))
    D = D_MODEL
    L = DEPTH

    def nrm(shape, scale):
        return jax.random.normal(next(ks), shape, jnp.float32) * scale

    def gain(shape):
        return 1.0 + nrm(shape, 0.02)

    return {
        'x': nrm((BATCH, SEQ, D), 1.0),
        'c': nrm((BATCH, D), 1.0),
        'ctx': nrm((BATCH, CTX_LEN, D), 1.0),
        'c_ctx': nrm((D,), 1.0),
        'w_mod': nrm((L, D, N_MOD * D), 0.5 * D ** -0.5),
        'b_mod': nrm((L, N_MOD * D), 0.02),
        'g_ffn1': gain((L, D)),
        'w13_ffn1': nrm((L, D, 2 * D_FF), D ** -0.5),
        'w2_ffn1': nrm((L, D_FF, D), D_FF ** -0.5),
        'g_mix': gain((L, D)),
        'w_in': nrm((L, D, D_IN), D ** -0.5),
        'g_q': gain((L, Q_LORA)),
        'w_uq': nrm((L, Q_LORA, MLA_HEADS * (QK_NOPE + QK_ROPE)), Q_LORA ** -0.5),
        'g_kv': gain((L, KV_LORA)),
        'w_ukv': nrm((L, KV_LORA, MLA_HEADS * (QK_NOPE + V_HEAD)), KV_LORA ** -0.5),
        'w_attn_o': nrm((L, MLA_HEADS * V_HEAD, D), (MLA_HEADS * V_HEAD) ** -0.5),
        'hy_conv_w': nrm((L, HY_SHORT, 3 * HY_WIDTH), HY_SHORT ** -0.5),
        'hy_conv_b': nrm((L, 3 * HY_WIDTH), 0.02),
        'hy_w1': nrm((L, HY_EMB, HY_FILTER_W), HY_EMB ** -0.5),
        'hy_b1': nrm((L, HY_FILTER_W), 0.02),
        'hy_w2': nrm((L, HY_FILTER_W, HY_FILTER_W), HY_FILTER_W ** -0.5),
        'hy_b2': nrm((L, HY_FILTER_W), 0.02),
        'hy_w3': nrm((L, HY_FILTER_W, HY_ORDER * HY_WIDTH), HY_FILTER_W ** -0.5),
        'hy_freq': gain((L, HY_FILTER_W)),
        'hy_skip': nrm((L, HY_ORDER, HY_WIDTH), 1.0),
        'w_hy_o': nrm((L, HY_WIDTH, D), HY_WIDTH ** -0.5),
        'w_out': nrm((L, D, D), D ** -0.5),
        'g_ffn2': gain((L, D)),
        'w13_ffn2': nrm((L, D, 2 * D_FF), D ** -0.5),
        'w2_ffn2': nrm((L, D_FF, D), D_FF ** -0.5),
        'g_final': gain((D,)),
    }


def reference(x, c, ctx, c_ctx, w_mod, b_mod, g_ffn1, w13_ffn1, w2_ffn1, g_mix, w_in,
              g_q, w_uq, g_kv, w_ukv, w_attn_o, hy_conv_w, hy_conv_b, hy_w1, hy_b1,
              hy_w2, hy_b2, hy_w3, hy_freq, hy_skip, w_hy_o, w_out, g_ffn2, w13_ffn2,
              w2_ffn2, g_final):
    B, S, _ = x.shape
    CL = ctx.shape[1]
    ROWS = S // GRID_W
    rows = jnp.repeat(jnp.arange(ROWS, dtype=jnp.float32), GRID_W)
    cols = jnp.tile(jnp.arange(GRID_W, dtype=jnp.float32), ROWS)
    inv_freq = ROPE_THETA ** (-jnp.arange(0, ROPE_AXIS, 2, dtype=jnp.float32) / ROPE_AXIS)
    ang_row = rows[:, None] * inv_freq
    ang_col = cols[:, None] * inv_freq

    for li in range(DEPTH):
        last = li == DEPTH - 1
        mx = (jax.nn.silu(c) @ w_mod[li] + b_mod[li]).reshape(B, N_MOD, 1, D_MODEL)
        mc = (jax.nn.silu(c_ctx) @ w_mod[li] + b_mod[li]).reshape(N_MOD, D_MODEL)

        x = half_ffn(x, mx[:, 0], mx[:, 1], mx[:, 2], g_ffn1[li], w13_ffn1[li], w2_ffn1[li])
        ctx = half_ffn(ctx, mc[0], mc[1], mc[2], g_ffn1[li], w13_ffn1[li], w2_ffn1[li])

        hx = modulate(rms_norm(x, g_mix[li]), mx[:, 3], mx[:, 4])
        hc = modulate(rms_norm(ctx, g_mix[li]), mc[3], mc[4])
        xq, xkv, xkr, xhy, xgate = split_proj(hx @ w_in[li])
        if last:
            ckvr = hc @ w_in[li][:, Q_END:KR_END]
            ckv, ckr = ckvr[..., :KV_LORA], ckvr[..., KV_LORA:]
        else:
            cq, ckv, ckr, chy, cgate = split_proj(hc @ w_in[li])

        q_nope, q_rope = mla_query(xq, g_q[li], w_uq[li])
        q_lat = jnp.concatenate([q_nope, rope_2d(q_rope, ang_row[:, None], ang_col[:, None])], axis=-1)
        k_nope, v_lat = mla_kv(xkv, g_kv[li], w_ukv[li])
        k_lat = join_key(k_nope, rope_2d(xkr, ang_row, ang_col))
        kc_nope, v_ctx = mla_kv(ckv, g_kv[li], w_ukv[li])
        k_ctx = join_key(kc_nope, ckr)
        k_all = jnp.concatenate([k_lat, k_ctx], axis=1)
        v_all = jnp.concatenate([v_lat, v_ctx], axis=1)
        attn_x = blocked_attention(q_lat, k_all, v_all).reshape(B, S, MLA_HEADS * V_HEAD)

        filt_x = hyena_filters(S, hy_w1[li], hy_b1[li], hy_w2[li], hy_b2[li], hy_w3[li], hy_freq[li])
        hy_x = hyena(xhy, hy_conv_w[li], hy_conv_b[li], filt_x, hy_skip[li])
        x_mix = merge_branches(attn_x, hy_x, xgate, w_attn_o[li], w_hy_o[li], w_out[li])

        if not last:
            cq_nope, cq_rope = mla_query(cq, g_q[li], w_uq[li])
            q_ctx = jnp.concatenate([cq_nope, cq_rope], axis=-1)
            attn_c = softmax_attend(q_ctx, k_ctx, v_ctx).reshape(B, CL, MLA_HEADS * V_HEAD)
            filt_c = hyena_filters(CL, hy_w1[li], hy_b1[li], hy_w2[li], hy_b2[li], hy_w3[li], hy_freq[li])
            hy_c = hyena(chy, hy_conv_w[li], hy_conv_b[li], filt_c, hy_skip[li])
            ctx = ctx + mc[5] * merge_branches(attn_c, hy_c, cgate, w_attn_o[li], w_hy_o[li], w_out[li])
            ctx = half_ffn(ctx, mc[6], mc[7], mc[8], g_ffn2[li], w13_ffn2[li], w2_ffn2[li])

        x = x + mx[:, 5] * x_mix
        x = half_ffn(x, mx[:, 6], mx[:, 7], mx[:, 8], g_ffn2[li], w13_ffn2[li], w2_ffn2[li])

    return rms_norm(x, g_final)
```

```python
import contextlib
import math
import numpy as np
import ml_dtypes
import concourse.bass as bass
import concourse.mybir as mybir
from concourse.bass_utils import run_bass_kernel_spmd

F32 = mybir.dt.float32
BF16 = mybir.dt.bfloat16
ALU = mybir.AluOpType
AF = mybir.ActivationFunctionType
AX = mybir.AxisListType

PE, DVE, ACT, POOL, SP = "tensor", "vector", "scalar", "gpsimd", "sync"
ENGS = [PE, DVE, ACT, POOL, SP]

NCORES = 8
D = 2048
KC = 16
DFF = 5632
NJ = 44
SEQ = 8192
TL = 1024
CTXL = 32
NCOL = TL + CTXL + 2
NKEY = NCORES * (TL + CTXL)
EPS = 1e-6
ATTN_SCALE = 192 ** -0.5
DEBUG_A = False


class Buf:
    def __init__(self, name):
        self.name = name
        self.writers = []
        self.readers = []
        self.sem = None


class Op:
    __slots__ = ("eng", "fn", "deps", "is_dma", "sem", "val", "needs_inc", "inc_amt")

    def __init__(self, eng, fn):
        self.eng = eng
        self.fn = fn
        self.deps = []
        self.is_dma = False
        self.sem = None
        self.val = None
        self.needs_inc = False
        self.inc_amt = 1


class Prog:
    def __init__(self, nc, n_dma_sems=80, same_engine_sync=True):
        self.nc = nc
        self.es = contextlib.ExitStack()
        self.ops = []
        self.same_engine_sync = same_engine_sync
        self.eng_sem = {e: self.es.enter_context(nc.semaphore("s_" + e)) for e in ENGS}
        self.dma_sems = [self.es.enter_context(nc.semaphore("d%d" % i)) for i in range(n_dma_sems)]
        self.dma_sem_cnt = [0] * n_dma_sems
        self.free_dma = list(range(n_dma_sems))
        self.phase_bufs = []

    def buf(self, name="b"):
        b = Buf(name)
        self.phase_bufs.append(b)
        return b

    def bufs(self, n, name="b"):
        return [self.buf(name + str(i)) for i in range(n)]

    def _deps(self, op, reads, writes):
        deps = []
        for r in reads:
            deps += r.writers
        for w in writes:
            deps += w.writers
            deps += w.readers
        seen = set()
        for d in deps:
            if id(d) in seen or d is op:
                continue
            seen.add(id(d))
            if d.eng == op.eng and not d.is_dma and not op.is_dma:
                if op.eng == PE or not self.same_engine_sync:
                    continue
            op.deps.append(d)
            d.needs_inc = True

    def op(self, eng, fn, reads=(), writes=()):
        o = Op(eng, fn)
        self._deps(o, reads, writes)
        for r in reads:
            r.readers.append(o)
        for w in writes:
            w.writers = [o]
            w.readers = []
        self.ops.append(o)
        return o

    def dma(self, eng, fn, reads=(), writes=()):
        o = Op(eng, fn)
        o.is_dma = True
        o.needs_inc = True
        o.inc_amt = 16
        dst = writes[0]
        if dst.sem is None:
            dst.sem = self.free_dma.pop(0)
        si = dst.sem
        self._deps(o, reads, writes)
        self.dma_sem_cnt[si] += 16
        o.sem = si
        o.val = self.dma_sem_cnt[si]
        for r in reads:
            r.readers.append(o)
        for w in writes:
            if w.readers or any(not x.is_dma for x in w.writers):
                w.writers = [o]
                w.readers = []
            else:
                w.writers.append(o)
        self.ops.append(o)
        return o

    def barrier(self):
        last = {}
        for o in self.ops:
            if o.fn is None:
                continue
            if o.is_dma:
                last[("d", o.sem)] = o
            else:
                last[("e", o.eng)] = o
        deps = list(last.values())
        for e in ENGS:
            o = Op(e, None)
            for d in deps:
                o.deps.append(d)
                d.needs_inc = True
            self.ops.append(o)
        for b in self.phase_bufs:
            if b.sem is not None:
                self.free_dma.append(b.sem)
                b.sem = None
            b.writers = []
            b.readers = []
        self.phase_bufs = []

    def emit(self):
        nc = self.nc
        cnt = {e: 0 for e in ENGS}
        for o in self.ops:
            if not o.is_dma and o.needs_inc:
                cnt[o.eng] += 1
                o.val = cnt[o.eng]
        per = {e: [o for o in self.ops if o.eng == e] for e in ENGS}

        def run(engname, eng):
            waited = {}
            for o in per[engname]:
                for d in o.deps:
                    if d.is_dma:
                        key, sem = ("d", d.sem), self.dma_sems[d.sem]
                    else:
                        key, sem = ("e", d.eng), self.eng_sem[d.eng]
                    if waited.get(key, 0) >= d.val:
                        continue
                    waited[key] = d.val
                    eng.wait_ge(sem, d.val)
                if o.fn is None:
                    continue
                ins = o.fn(eng)
                if o.is_dma:
                    ins.then_inc(self.dma_sems[o.sem], 16)
                elif o.needs_inc:
                    ins.then_inc(self.eng_sem[engname], 1)

        with nc.Block() as block:
            @block.tensor
            def _(e):
                run(PE, e)

            @block.vector
            def _(e):
                run(DVE, e)

            @block.scalar
            def _(e):
                run(ACT, e)

            @block.gpsimd
            def _(e):
                run(POOL, e)

            @block.sync
            def _(e):
                run(SP, e)
        self.es.close()


class Ctx:
    def __init__(self):
        self.nc = bass.Bass("TRN2", target_bir_lowering=False)
        self.P = Prog(self.nc)
        self.stacks = []
        self.n = 0
        self.push()
        self.ps = [self.stack.enter_context(self.nc.psum_tensor("ps%d" % i, [128, 512], F32)) for i in range(8)]
        self.psb = self.P.bufs(8, "ps")
        self.P.phase_bufs = [b for b in self.P.phase_bufs if b not in self.psb]
        self.rot = {}

    @property
    def stack(self):
        return self.stacks[-1]

    def push(self):
        self.stacks.append(contextlib.ExitStack())

    def pop(self):
        self.P.barrier()
        for b in self.psb:
            b.writers = []
            b.readers = []
        self.stacks.pop().close()

    def sb(self, shape, dt, name=None):
        self.n += 1
        t = self.stack.enter_context(self.nc.sbuf_tensor("%s_%d" % (name or "t", self.n), list(shape), dt))
        return t, self.P.buf(name or "t")

    def din(self, name, shape, dt=F32):
        return self.nc.dram_tensor(name, list(shape), dt, kind="ExternalInput"), self.P.buf(name)

    def dout(self, name, shape, dt=F32):
        return self.nc.dram_tensor(name, list(shape), dt, kind="ExternalOutput"), self.P.buf(name)

    def pick(self, key, items):
        i = self.rot.get(key, 0)
        self.rot[key] = i + 1
        return items[i % len(items)]

    def finish(self):
        while self.stacks:
            self.pop() if len(self.stacks) > 1 else (self.P.barrier(), self.stacks.pop().close())
        self.P.emit()
        return self.nc


def dap(t, off, dims):
    return bass.AP(t, off, [list(d) for d in dims])


TILES = [(0, 512), (512, 512), (1024, NCOL - 1024)]


def load_consts(C, names_shapes):
    out = {}
    for name, shape in names_shapes:
        d, db = C.din(name, shape)
        t, tb = C.sb(shape, F32, name)
        ap_in = d[tuple(slice(None) for _ in shape)]
        C.P.dma(SP, lambda e, t=t, ap_in=ap_in: e.dma_start(out=t[:], in_=ap_in), reads=[db], writes=[tb])
        out[name] = (t, tb)
    return out


def rms_modulate(C, xT, xTb, hT, hTb, ncols, gT, modT, modb, m_shift, m_scale, ones, onesb, tmp_pool, ctx_lo=None, ctx_hi=None):
    P, nc = C.P, C.nc
    tiles = [(s, min(n, ncols - s)) for (s, n) in TILES if s < ncols]
    A, Ab = C.sb([128, KC, 2], F32, "A")
    P.op(DVE, lambda e: e.tensor_scalar(out=A[:], in0=modT[:, m_scale * KC:(m_scale + 1) * KC, :], scalar1=1.0, scalar2=None, op0=ALU.add), reads=[modb], writes=[Ab])
    P.op(DVE, lambda e: e.tensor_tensor(out=A[:], in0=A[:], in1=gT[0][:, :, None].to_broadcast([128, KC, 2]), op=ALU.mult), reads=[Ab, gT[1]], writes=[Ab])
    rstd, rstdb = C.sb([128, NCOL], F32, "rstd")
    sqs = tmp_pool
    accs = [(C.ps[5 + i], C.psb[5 + i]) for i in range(3)]
    for kc in range(KC):
        sq, sqb = sqs[kc % 2]
        P.op(ACT, lambda e, sq=sq, kc=kc: e.activation(out=sq[:, :ncols], in_=xT[:, kc, :ncols], func=AF.Square), reads=[xTb], writes=[sqb])
        for ti, (s, n) in enumerate(tiles):
            ps, psb = accs[ti]
            P.op(PE, lambda e, ps=ps, sq=sq, s=s, n=n, kc=kc: e.matmul(ps[:, :n], ones[:], sq[:, s:s + n], start=(kc == 0), stop=(kc == KC - 1)), reads=[sqb, onesb], writes=[psb])
    for ti, (s, n) in enumerate(tiles):
        ps, psb = accs[ti]
        P.op(ACT, lambda e, ps=ps, s=s, n=n: e.activation(out=rstd[:, s:s + n], in_=ps[:, :n], func=AF.Sqrt, scale=1.0 / D, bias=C.epsb[0][:, 0:1]), reads=[psb, C.epsb[1]], writes=[rstdb])
    P.op(DVE, lambda e: e.reciprocal(out=rstd[:, :ncols], in_=rstd[:, :ncols]), reads=[rstdb], writes=[rstdb])
    for kc in range(KC):
        t, tb = tmp_pool[kc % len(tmp_pool)]
        P.op(DVE, lambda e, t=t, kc=kc: e.tensor_tensor(out=t[:, :ncols], in0=xT[:, kc, :ncols], in1=rstd[:, :ncols], op=ALU.mult), reads=[xTb, rstdb], writes=[tb])
        segs = [(0, ncols, 0)]
        if ctx_lo is not None:
            segs = [(0, ctx_lo, 0), (ctx_lo, ctx_hi, 1)] + ([(ctx_hi, ncols, 0)] if ctx_hi < ncols else [])
        for (a, b, which) in segs:
            P.op(ACT, lambda e, t=t, kc=kc, a=a, b=b, which=which: e.activation(
                out=hT[:, kc, a:b], in_=t[:, a:b], func=AF.Identity,
                scale=A[:, kc, which:which + 1], bias=modT[:, m_shift * KC + kc, which:which + 1]),
                reads=[tb, Ab, modb], writes=[hTb])


def ffn(C, xT, xTb, hT, hTb, w13, w13b, w2, w2b, modT, modb, m_gate, slots, ncols, ctx_lo, ctx_hi, tmp_pool):
    P = C.P
    tiles = [(s, min(n, ncols - s)) for (s, n) in TILES if s < ncols]
    hg, hgb = C.sb([128, KC, 2], F32, "hg")
    P.op(DVE, lambda e: e.tensor_scalar(out=hg[:], in0=modT[:, m_gate * KC:(m_gate + 1) * KC, :], scalar1=0.5, scalar2=None, op0=ALU.mult), reads=[modb], writes=[hgb])
    GJ = 11
    gT, _ = C.sb([128, GJ, NCOL], BF16, "gT")
    gTb = C.P.bufs(GJ, "gT")
    w2s = [C.sb([128, GJ, 128], BF16, "w2s") for _ in range(2)]
    sas = tmp_pool
    for j in range(NJ):
        jj = j % GJ
        wa, wab = C.pick("slot", slots)
        wb, wbb = C.pick("slot", slots)
        P.dma(POOL, lambda e, wa=wa, j=j: e.dma_start(out=wa[:], in_=dap(w13, j * 128, [[2 * DFF, 128], [128 * 2 * DFF, KC], [1, 128]])), reads=[w13b], writes=[wab])
        P.dma(POOL, lambda e, wb=wb, j=j: e.dma_start(out=wb[:], in_=dap(w13, DFF + j * 128, [[2 * DFF, 128], [128 * 2 * DFF, KC], [1, 128]])), reads=[w13b], writes=[wbb])
        cj = 0
        for (s, n) in tiles:
            pi = C.pick("ffnps", [0, 2])
            pa, pab, pb, pbb = C.ps[pi], C.psb[pi], C.ps[pi + 1], C.psb[pi + 1]
            for kc in range(KC):
                P.op(PE, lambda e, pa=pa, wa=wa, kc=kc, cj=cj, s=s, n=n: e.matmul(pa[:, :n], wa[:, kc, cj:cj + 128], hT[:, kc, s:s + n], start=(kc == 0), stop=(kc == KC - 1)), reads=[wab, hTb], writes=[pab])
            for kc in range(KC):
                P.op(PE, lambda e, pb=pb, wb=wb, kc=kc, cj=cj, s=s, n=n: e.matmul(pb[:, :n], wb[:, kc, cj:cj + 128], hT[:, kc, s:s + n], start=(kc == 0), stop=(kc == KC - 1)), reads=[wbb, hTb], writes=[pbb])
            sa, sab = C.pick("sa", sas)
            P.op(ACT, lambda e, sa=sa, pa=pa, n=n: e.activation(out=sa[:, :n], in_=pa[:, :n], func=AF.Silu), reads=[pab], writes=[sab])
            P.op(DVE, lambda e, sa=sa, pb=pb, jj=jj, s=s, n=n: e.tensor_tensor(out=gT[:, jj, s:s + n], in0=pb[:, :n], in1=sa[:, :n], op=ALU.mult), reads=[pbb, sab], writes=[gTb[jj]])
        if jj == GJ - 1:
            grp = j // GJ
            for oc in range(KC):
                ws, wsb = C.pick("w2s", w2s)
                P.dma(POOL, lambda e, ws=ws, grp=grp, oc=oc: e.dma_start(out=ws[:], in_=dap(w2, grp * GJ * 128 * D + oc * 128, [[D, 128], [128 * D, GJ], [1, 128]])), reads=[w2b], writes=[wsb])
                if True:
                    for (s, n) in tiles:
                        pi = C.pick("w2ps", [4, 5, 6, 7])
                        po, pob = C.ps[pi], C.psb[pi]
                        for k in range(GJ):
                            P.op(PE, lambda e, po=po, ws=ws, k=k, s=s, n=n: e.matmul(po[:, :n], ws[:, k, :], gT[:, k, s:s + n], start=(k == 0), stop=(k == GJ - 1)), reads=[wsb, gTb[k]], writes=[pob])
                        segs = []
                        for (a, b, which) in [(0, ctx_lo, 0), (ctx_lo, ctx_hi, 1), (ctx_hi, ncols, 0)]:
                            a2, b2 = max(a, s), min(b, s + n)
                            if a2 < b2:
                                segs.append((a2, b2, which))
                        for (a, b, which) in segs:
                            P.op(DVE, lambda e, po=po, oc=oc, a=a, b=b, s=s, which=which: e.scalar_tensor_tensor(
                                out=xT[:, oc, a:b], in0=po[:, a - s:b - s], scalar=hg[:, oc, which:which + 1], in1=xT[:, oc, a:b], op0=ALU.mult, op1=ALU.add),
                                reads=[pob, hgb, xTb], writes=[xTb])


def common_consts(C):
    P, nc = C.P, C.nc
    C.ones, C.onesb = C.sb([128, 128], F32, "ones")
    P.op(DVE, lambda e: e.memset(C.ones[:], 1.0), writes=[C.onesb])
    eps, epsb = C.sb([128, 1], F32, "eps")
    P.op(DVE, lambda e: e.memset(eps[:], EPS), writes=[epsb])
    C.epsb = (eps, epsb)
    cs = load_consts(C, [("ident", [128, 128])])
    C.ident, C.identb = cs["ident"]


def rms_small(C, pT, pTb, ncols, gT, gTb, outT, outTb, sq_pool, tiles):
    P = C.P
    rstd, rstdb = C.sb([128, TL + CTXL], F32, "rstds")
    accs = [(C.ps[5 + i], C.psb[5 + i]) for i in range(3)]
    for ch in range(4):
        sq, sqb = sq_pool[ch % 2]
        P.op(ACT, lambda e, sq=sq, ch=ch: e.activation(out=sq[:, :ncols], in_=pT[:, ch, :ncols], func=AF.Square), reads=[pTb], writes=[sqb])
        for ti, (s, n) in enumerate(tiles):
            ps, psb = accs[ti]
            P.op(PE, lambda e, ps=ps, sq=sq, s=s, n=n, ch=ch: e.matmul(ps[:, :n], C.ones[:], sq[:, s:s + n], start=(ch == 0), stop=(ch == 3)), reads=[sqb, C.onesb], writes=[psb])
    for ti, (s, n) in enumerate(tiles):
        ps, psb = accs[ti]
        P.op(ACT, lambda e, ps=ps, s=s, n=n: e.activation(out=rstd[:, s:s + n], in_=ps[:, :n], func=AF.Sqrt, scale=1.0 / 512, bias=C.epsb[0][:, 0:1]), reads=[psb, C.epsb[1]], writes=[rstdb])
    P.op(DVE, lambda e: e.reciprocal(out=rstd[:, :ncols], in_=rstd[:, :ncols]), reads=[rstdb], writes=[rstdb])
    for ch in range(4):
        t, tb = sq_pool[ch % 2]
        P.op(DVE, lambda e, t=t, ch=ch: e.tensor_tensor(out=t[:, :ncols], in0=pT[:, ch, :ncols], in1=rstd[:, :ncols], op=ALU.mult), reads=[pTb, rstdb], writes=[tb])
        P.op(ACT, lambda e, t=t, ch=ch: e.activation(out=outT[:, ch, :ncols], in_=t[:, :ncols], func=AF.Identity, scale=gT[:, ch:ch + 1]), reads=[tb, gTb], writes=[outTb])
    return rstd, rstdb


def build_A2(C, S):
    P = C.P
    cs, hT, hTb, slots = S["cs"], S["hT"], S["hTb"], S["slots"]
    pqT, pqTb, pkvT, pkvTb = S["pqT"], S["pqTb"], S["pkvT"], S["pkvTb"]
    w_in, w_inb = S["w_in"], S["w_inb"]
    cos, cosb = cs["rope_cos"]
    sin, sinb = cs["rope_sin"]
    hmask, hmaskb = cs["hmask"]
    LT = [(0, 512), (512, 512)]
    CT = (TL, CTXL)
    HT = (TL + CTXL, 2)

    def proj(col_pieces, width, tiles, evac):
        w, wb = C.pick("slot", slots)
        for (dst0, src0, wd) in col_pieces:
            P.dma(POOL, lambda e, w=w, dst0=dst0, src0=src0, wd=wd: e.dma_start(out=w[:, :, dst0:dst0 + wd], in_=dap(w_in, src0, [[8256, 128], [128 * 8256, KC], [1, wd]])), reads=[w_inb], writes=[wb])
        for (s, n) in tiles:
            pi = C.pick("prps", [0, 1, 2, 3])
            ps, psb = C.ps[pi], C.psb[pi]
            for kc in range(KC):
                P.op(PE, lambda e, ps=ps, w=w, kc=kc, s=s, n=n: e.matmul(ps[:width, :n], w[:, kc, :width], hT[:, kc, s:s + n], start=(kc == 0), stop=(kc == KC - 1)), reads=[wb, hTb], writes=[psb])
            evac(ps, psb, s, n)

    C.push()
    for ch in range(4):
        proj([(0, ch * 128, 128)], 128, LT, lambda ps, psb, s, n, ch=ch: P.op(ACT, lambda e: e.activation(out=pqT[:, ch, s:s + n], in_=ps[:, :n], func=AF.Copy), reads=[psb], writes=[pqTb]))
    for ch in range(4):
        proj([(0, 512 + ch * 128, 128)], 128, LT + [CT], lambda ps, psb, s, n, ch=ch: P.op(ACT, lambda e: e.activation(out=pkvT[:, ch, s:s + n], in_=ps[:, :n], func=AF.Copy), reads=[psb], writes=[pkvTb]))
    krT, krTb = C.sb([64, TL + CTXL], BF16, "krT")
    kt1, kt1b = C.sb([64, TL], F32, "kt1")
    kt2, kt2b = C.sb([64, TL], F32, "kt2")

    def ev_kr1(ps, psb, s, n):
        if s >= TL:
            P.op(ACT, lambda e: e.activation(out=krT[:, s:s + n], in_=ps[:64, :n], func=AF.Copy), reads=[psb], writes=[krTb])
        else:
            P.op(DVE, lambda e: e.tensor_tensor(out=kt1[:, s:s + n], in0=ps[:64, :n], in1=cos[:, s:s + n], op=ALU.mult), reads=[psb, cosb], writes=[kt1b])

    def ev_kr2(ps, psb, s, n):
        P.op(DVE, lambda e: e.tensor_tensor(out=kt2[:, s:s + n], in0=ps[:64, :n], in1=sin[:, s:s + n], op=ALU.mult), reads=[psb, sinb], writes=[kt2b])

    proj([(0, 1024, 64)], 64, LT + [CT], ev_kr1)
    proj([(0, 1024 + 16, 16), (16, 1024, 16), (32, 1024 + 48, 16), (48, 1024 + 32, 16)], 64, LT, ev_kr2)
    P.op(DVE, lambda e: e.tensor_tensor(out=krT[:, :TL], in0=kt1[:], in1=kt2[:], op=ALU.add), reads=[kt1b, kt2b], writes=[krTb])
    P.dma(SP, lambda e: e.dma_start(out=S["o_kr"][:, :], in_=krT[:]), reads=[krTb], writes=[S["o_krb"]])
    cw, cwb = cs["convwT"]
    cb, cbb = cs["convbT"]
    hybs = [C.sb([128, TL + 2], F32, "hyb") for _ in range(2)]
    hyos = [C.sb([128, TL], F32, "hyo") for _ in range(2)]
    for ch in range(24):
        hyb, hybb = C.pick("hyb", hybs)
        hyo, hyob = C.pick("hyo", hyos)

        def ev_hy(ps, psb, s, n, hyb=hyb, hybb=hybb):
            if s >= TL:
                P.op(DVE, lambda e: e.tensor_tensor(out=hyb[:, 0:1], in0=ps[:, 0:1], in1=hmask[:, 0:1], op=ALU.mult), reads=[psb, hmaskb], writes=[hybb])
                P.op(DVE, lambda e: e.tensor_tensor(out=hyb[:, TL + 1:TL + 2], in0=ps[:, 1:2], in1=hmask[:, 1:2], op=ALU.mult), reads=[psb, hmaskb], writes=[hybb])
            else:
                P.op(ACT, lambda e: e.activation(out=hyb[:, 1 + s:1 + s + n], in_=ps[:, :n], func=AF.Copy), reads=[psb], writes=[hybb])

        proj([(0, 1088 + ch * 128, 128)], 128, LT + [HT], ev_hy)
        P.op(ACT, lambda e, hyb=hyb, hyo=hyo, ch=ch: e.activation(out=hyo[:], in_=hyb[:, 1:TL + 1], func=AF.Identity, scale=cw[:, ch, 1:2], bias=cb[:, ch:ch + 1]), reads=[hybb, cwb, cbb], writes=[hyob])
        P.op(DVE, lambda e, hyb=hyb, hyo=hyo, ch=ch: e.scalar_tensor_tensor(out=hyo[:], in0=hyb[:, 0:TL], scalar=cw[:, ch, 0:1], in1=hyo[:], op0=ALU.mult, op1=ALU.add), reads=[hybb, cwb, hyob], writes=[hyob])
        P.op(DVE, lambda e, hyb=hyb, hyo=hyo, ch=ch: e.scalar_tensor_tensor(out=hyo[:], in0=hyb[:, 2:TL + 2], scalar=cw[:, ch, 2:3], in1=hyo[:], op0=ALU.mult, op1=ALU.add), reads=[hybb, cwb, hyob], writes=[hyob])
        P.dma(SP, lambda e, hyo=hyo, ch=ch: e.dma_start(out=S["o_hy"][ch, :, :], in_=hyo[:]), reads=[hyob], writes=[S["o_hyb"]])
    C.pop()
    C.pop()

    C.push()
    tmp_pool = S["tmp_pool"]
    hqT, hqTb = C.sb([128, 4, TL], BF16, "hqT")
    hkvT, hkvTb = C.sb([128, 4, TL + CTXL], BF16, "hkvT")
    rq, rqb = rms_small(C, pqT, pqTb, TL, cs["gqT"][0], cs["gqT"][1], hqT, hqTb, tmp_pool, LT)
    if DEBUG_A:
        drs, drsb = C.dout("dbg_rstd", [128, TL + CTXL])
        P.dma(SP, lambda e: e.dma_start(out=drs[:, :], in_=rq[:]), reads=[rqb], writes=[drsb])
        dpq, dpqb = C.dout("dbg_pqT", [128, 4, TL])
        dhq, dhqb = C.dout("dbg_hqT", [128, 4, TL], BF16)
        P.dma(SP, lambda e: e.dma_start(out=dpq[:, :, :], in_=pqT[:]), reads=[pqTb], writes=[dpqb])
        P.dma(SP, lambda e: e.dma_start(out=dhq[:, :, :], in_=hqT[:]), reads=[hqTb], writes=[dhqb])
    rms_small(C, pkvT, pkvTb, TL + CTXL, cs["gkvT"][0], cs["gkvT"][1], hkvT, hkvTb, tmp_pool, LT + [CT])
    w_uq, w_uqb, w_ukv, w_ukvb = S["w_uq"], S["w_uqb"], S["w_ukv"], S["w_ukvb"]
    wuq, wuqb = C.sb([128, 4, 3072], BF16, "wuq")
    P.dma(POOL, lambda e: e.dma_start(out=wuq[:], in_=dap(w_uq, 0, [[3072, 128], [128 * 3072, 4], [1, 3072]])), reads=[w_uqb], writes=[wuqb])
    wsw, wswb = C.sb([128, 4, 16, 64], BF16, "wsw")
    for kc in range(4):
        for (dst0, src0) in [(0, 16), (16, 0), (32, 48), (48, 32)]:
            P.dma(POOL, lambda e, kc=kc, dst0=dst0, src0=src0: e.dma_start(out=wsw[:, kc, :, dst0:dst0 + 16], in_=dap(w_uq, kc * 128 * 3072 + 128 + src0, [[3072, 128], [192, 16], [1, 16]])), reads=[w_uqb], writes=[wswb])
    wukv, wukvb = C.sb([128, 4, 4096], BF16, "wukv")
    P.dma(POOL, lambda e: e.dma_start(out=wukv[:], in_=dap(w_ukv, 0, [[4096, 128], [128 * 4096, 4], [1, 4096]])), reads=[w_ukvb], writes=[wukvb])
    qns = [C.sb([128, TL], BF16, "qn") for _ in range(2)]
    qrs = [C.sb([64, TL], BF16, "qr") for _ in range(2)]
    qt1, qt1b = C.sb([64, TL], F32, "qt1")
    qt2, qt2b = C.sb([64, TL], F32, "qt2")
    for h in range(16):
        qn, qnb = C.pick("qn", qns)
        qr, qrb = C.pick("qr", qrs)
        for (s, n) in LT:
            pi = C.pick("prps", [0, 1, 2, 3])
            ps, psb = C.ps[pi], C.psb[pi]
            for kc in range(4):
                P.op(PE, lambda e, ps=ps, kc=kc, s=s, n=n, h=h: e.matmul(ps[:, :n], wuq[:, kc, h * 192:h * 192 + 128], hqT[:, kc, s:s + n], start=(kc == 0), stop=(kc == 3)), reads=[wuqb, hqTb], writes=[psb])
            P.op(ACT, lambda e, ps=ps, qn=qn, s=s, n=n: e.activation(out=qn[:, s:s + n], in_=ps[:, :n], func=AF.Copy), reads=[psb], writes=[qnb])
            pi = C.pick("prps", [0, 1, 2, 3])
            ps1, ps1b = C.ps[pi], C.psb[pi]
            for kc in range(4):
                P.op(PE, lambda e, ps1=ps1, kc=kc, s=s, n=n, h=h: e.matmul(ps1[:64, :n], wuq[:, kc, h * 192 + 128:h * 192 + 192], hqT[:, kc, s:s + n], start=(kc == 0), stop=(kc == 3)), reads=[wuqb, hqTb], writes=[ps1b])
            P.op(DVE, lambda e, ps1=ps1, s=s, n=n: e.tensor_tensor(out=qt1[:, s:s + n], in0=ps1[:64, :n], in1=cos[:, s:s + n], op=ALU.mult), reads=[ps1b, cosb], writes=[qt1b])
            pi = C.pick("prps", [0, 1, 2, 3])
            ps2, ps2b = C.ps[pi], C.psb[pi]
            for kc in range(4):
                P.op(PE, lambda e, ps2=ps2, kc=kc, s=s, n=n, h=h: e.matmul(ps2[:64, :n], wsw[:, kc, h, :], hqT[:, kc, s:s + n], start=(kc == 0), stop=(kc == 3)), reads=[wswb, hqTb], writes=[ps2b])
            P.op(DVE, lambda e, ps2=ps2, s=s, n=n: e.tensor_tensor(out=qt2[:, s:s + n], in0=ps2[:64, :n], in1=sin[:, s:s + n], op=ALU.mult), reads=[ps2b, sinb], writes=[qt2b])
        P.op(DVE, lambda e, qr=qr: e.tensor_tensor(out=qr[:], in0=qt1[:], in1=qt2[:], op=ALU.add), reads=[qt1b, qt2b], writes=[qrb])
        P.dma(SP, lambda e, qn=qn, h=h: e.dma_start(out=S["o_q"][h, 0:128, :], in_=qn[:]), reads=[qnb], writes=[S["o_qb"]])
        P.dma(SP, lambda e, qr=qr, h=h: e.dma_start(out=S["o_q"][h, 128:192, :], in_=qr[:]), reads=[qrb], writes=[S["o_qb"]])
    kns = [C.sb([128, TL + CTXL], BF16, "kn") for _ in range(2)]
    for h in range(16):
        kn, knb = C.pick("kn", kns)
        for (s, n) in LT + [CT]:
            pi = C.pick("prps", [0, 1, 2, 3])
            ps, psb = C.ps[pi], C.psb[pi]
            for kc in range(4):
                P.op(PE, lambda e, ps=ps, kc=kc, s=s, n=n, h=h: e.matmul(ps[:, :n], wukv[:, kc, h * 256:h * 256 + 128], hkvT[:, kc, s:s + n], start=(kc == 0), stop=(kc == 3)), reads=[wukvb, hkvTb], writes=[psb])
            P.op(ACT, lambda e, ps=ps, kn=kn, s=s, n=n: e.activation(out=kn[:, s:s + n], in_=ps[:, :n], func=AF.Copy), reads=[psb], writes=[knb])
        P.dma(SP, lambda e, kn=kn, h=h: e.dma_start(out=S["o_kn"][h, :, :], in_=kn[:]), reads=[knb], writes=[S["o_knb"]])
    vts = [C.sb([128, 2048], BF16, "vt") for _ in range(2)]
    for tb in range(9):
        t0 = tb * 128
        nt = min(128, TL + CTXL - t0)
        vt, vtb = C.pick("vt", vts)
        for hg in range(4):
            pi = C.pick("prps", [0, 1, 2, 3])
            ps, psb = C.ps[pi], C.psb[pi]
            for kc in range(4):
                P.op(PE, lambda e, ps=ps, kc=kc, hg=hg, t0=t0, nt=nt: e.matmul(
                    ps[:nt, :], hkvT[:, kc, t0:t0 + nt], wukv[:, kc, hg * 1024:(hg + 1) * 1024].rearrange("p (h x) -> p h x", x=256)[:, :, 128:256],
                    start=(kc == 0), stop=(kc == 3)), reads=[wukvb, hkvTb], writes=[psb])
            P.op(ACT, lambda e, ps=ps, vt=vt, hg=hg, nt=nt: e.activation(out=vt[:nt, hg * 512:(hg + 1) * 512], in_=ps[:nt, :], func=AF.Copy), reads=[psb], writes=[vtb])
        P.dma(SP, lambda e, vt=vt, t0=t0, nt=nt: e.dma_start(out=S["o_v"][t0:t0 + nt, :], in_=vt[:nt, :]), reads=[vtb], writes=[S["o_vb"]])
    C.pop()


def build_A():
    C = Ctx()
    P, nc = C.P, C.nc
    common_consts(C)
    cs = load_consts(C, [("cT", [128, KC, 2]), ("b_modT", [128, 9 * KC]), ("g1T", [128, KC]), ("gmixT", [128, KC]),
                         ("gqT", [128, 4]), ("gkvT", [128, 4]), ("convwT", [128, 24, 3]), ("convbT", [128, 24]),
                         ("rope_cos", [64, TL]), ("rope_sin", [64, TL]), ("hmask", [128, 2])])
    xin, xinb = C.din("xin", [NCOL, D])
    w_mod, w_modb = C.din("w_mod", [D, 9 * D])
    w13, w13b = C.din("w13", [D, 2 * DFF])
    w2, w2b = C.din("w2", [DFF, D])
    w_in, w_inb = C.din("w_in", [D, 8256])
    w_uq, w_uqb = C.din("w_uq", [512, 3072])
    w_ukv, w_ukvb = C.din("w_ukv", [512, 4096])
    o_xT, o_xTb = C.dout("o_xT", [128, KC, TL])
    o_hxT, o_hxTb = C.dout("o_hxT", [128, KC, TL], BF16)
    o_mod, o_modb = C.dout("o_mod", [128, 9 * KC, 2])
    o_q, o_qb = C.dout("o_q", [16, 192, TL], BF16)
    o_kn, o_knb = C.dout("o_kn", [16, 128, TL + CTXL], BF16)
    o_kr, o_krb = C.dout("o_kr", [64, TL + CTXL], BF16)
    o_v, o_vb = C.dout("o_v", [TL + CTXL, 2048], BF16)
    o_hy, o_hyb = C.dout("o_hy", [24, 128, TL])

    slots = [C.sb([128, KC, 128], BF16, "wslot") for _ in range(4)]
    pqT, pqTb = C.sb([128, 4, TL], F32, "pqT")
    pkvT, pkvTb = C.sb([128, 4, TL + CTXL], F32, "pkvT")
    modT, modb = C.sb([128, 9 * KC, 2], F32, "modT")
    tmp_pool = [C.sb([128, NCOL], F32, "tmp") for _ in range(2)]
    C.push()
    hT, hTb = C.sb([128, KC, NCOL], BF16, "hT")

    C.push()
    xT, xTb = C.sb([128, KC, NCOL], F32, "xT")
    sT, sTb = C.sb([128, KC, 2], BF16, "sT")
    cT, cTb = cs["cT"]
    P.op(ACT, lambda e: e.activation(out=sT[:], in_=cT[:], func=AF.Silu), reads=[cTb], writes=[sTb])
    pm, pmb = C.ps[7], C.psb[7]
    for g in range(9 * KC):
        wm, wmb = C.pick("slot", slots)
        P.dma(POOL, lambda e, wm=wm, g=g: e.dma_start(out=wm[:], in_=dap(w_mod, g * 128, [[9 * D, 128], [128 * 9 * D, KC], [1, 128]])), reads=[w_modb], writes=[wmb])
        c0 = g * 2
        for kc in range(KC):
            P.op(PE, lambda e, wm=wm, kc=kc, c0=c0: e.matmul(pm[:, c0:c0 + 2], wm[:, kc, :], sT[:, kc, :], start=(kc == 0), stop=(kc == KC - 1)), reads=[wmb, sTb], writes=[pmb])
    bm, bmb = cs["b_modT"]
    P.op(DVE, lambda e: e.tensor_tensor(out=modT[:], in0=pm[:, :288].rearrange("p (a b) -> p a b", b=2), in1=bm[:].unsqueeze(2).to_broadcast([128, 9 * KC, 2]), op=ALU.add), reads=[pmb, bmb], writes=[modb])
    P.dma(SP, lambda e: e.dma_start(out=o_mod[:, :, :], in_=modT[:]), reads=[modb], writes=[o_modb])
    C.push()
    xtoks = [C.sb([128, D], F32, "xtok") for _ in range(2)]
    for tb in range((NCOL + 127) // 128):
        r0 = tb * 128
        nr = min(128, NCOL - r0)
        xt, xtb = C.pick("xtok", xtoks)
        P.dma(SP, lambda e, xt=xt, r0=r0, nr=nr: e.dma_start(out=xt[:nr, :], in_=xin[r0:r0 + nr, :]), reads=[xinb], writes=[xtb])
        for q4 in range(4):
            pi = C.pick("trps", [0, 1, 2, 3])
            pt, ptb = C.ps[pi], C.psb[pi]
            for k in range(4):
                kc = q4 * 4 + k
                P.op(PE, lambda e, pt=pt, xt=xt, kc=kc, k=k, nr=nr: e.transpose(pt[:, k * 128:k * 128 + nr], xt[:nr, kc * 128:(kc + 1) * 128], C.ident[:nr, :nr]), reads=[xtb, C.identb], writes=[ptb])
            P.op(ACT, lambda e, pt=pt, q4=q4, r0=r0, nr=nr: e.activation(out=xT[:, q4 * 4:(q4 + 1) * 4, r0:r0 + nr], in_=pt[:, :].rearrange("p (a b) -> p a b", b=128)[:, :, :nr], func=AF.Copy), reads=[ptb], writes=[xTb])
    C.pop()
    g1 = cs["g1T"]
    rms_modulate(C, xT, xTb, hT, hTb, NCOL, g1, modT, modb, 0, 1, C.ones, C.onesb, tmp_pool, ctx_lo=TL, ctx_hi=TL + CTXL)
    C.push()
    ffn(C, xT, xTb, hT, hTb, w13, w13b, w2, w2b, modT, modb, 2, slots, NCOL, TL, TL + CTXL, tmp_pool)
    C.pop()
    P.dma(SP, lambda e: e.dma_start(out=o_xT[:, :, :], in_=xT[:, :, :TL]), reads=[xTb], writes=[o_xTb])
    rms_modulate(C, xT, xTb, hT, hTb, NCOL, cs["gmixT"], modT, modb, 3, 4, C.ones, C.onesb, tmp_pool, ctx_lo=TL, ctx_hi=TL + CTXL)
    P.dma(SP, lambda e: e.dma_start(out=o_hxT[:, :, :], in_=hT[:, :, :TL]), reads=[hTb], writes=[o_hxTb])
    C.pop()
    build_A2(C, dict(cs=cs, hT=hT, hTb=hTb, slots=slots, pqT=pqT, pqTb=pqTb, pkvT=pkvT, pkvTb=pkvTb, w_in=w_in, w_inb=w_inb, w_uq=w_uq, w_uqb=w_uqb, w_ukv=w_ukv, w_ukvb=w_ukvb,
                   o_q=o_q, o_qb=o_qb, o_kn=o_kn, o_knb=o_knb, o_kr=o_kr, o_krb=o_krb, o_v=o_v, o_vb=o_vb, o_hy=o_hy, o_hyb=o_hyb, tmp_pool=tmp_pool))
    return C


def _pl(v, nch):
    return np.ascontiguousarray(np.asarray(v, np.float32).reshape(nch, 128).T)


def _rope_tables(tok0, n):
    t = np.arange(tok0, tok0 + n)
    rows = (t // 64).astype(np.float32)
    cols = (t % 64).astype(np.float32)
    inv_freq = (np.float32(10000.0) ** (-np.arange(0, 32, 2, dtype=np.float32) / np.float32(32))).astype(np.float32)
    ang_r = rows[:, None] * inv_freq[None, :]
    ang_c = cols[:, None] * inv_freq[None, :]
    cos = np.zeros((64, n), np.float32)
    sin = np.zeros((64, n), np.float32)
    for i in range(64):
        ang = (ang_r if i < 32 else ang_c)[:, i % 16]
        cos[i] = np.cos(ang)
        sin[i] = np.sin(ang) * (-1.0 if (i % 32) < 16 else 1.0)
    return cos, sin


def prep_A(I):
    x = I["x"][0]
    ctx = I["ctx"][0]
    maps = []
    shared = dict(
        ident=np.eye(128, dtype=np.float32),
        cT=np.ascontiguousarray(np.stack([_pl(I["c"][0], KC), _pl(I["c_ctx"], KC)], axis=-1)),
        b_modT=_pl(I["b_mod"][0], 9 * KC), g1T=_pl(I["g_ffn1"][0], KC), gmixT=_pl(I["g_mix"][0], KC),
        gqT=_pl(I["g_q"][0], 4), gkvT=_pl(I["g_kv"][0], 4),
        convwT=np.ascontiguousarray(np.stack([_pl(I["hy_conv_w"][0][k], 24) for k in range(3)], axis=-1)),
        convbT=_pl(I["hy_conv_b"][0], 24),
        w_mod=I["w_mod"][0], w13=I["w13_ffn1"][0], w2=I["w2_ffn1"][0], w_in=I["w_in"][0], w_uq=I["w_uq"][0], w_ukv=I["w_ukv"][0],
    )
    zero = np.zeros((1, D), np.float32)
    for i in range(NCORES):
        lo, hi = i * TL, (i + 1) * TL
        xin = np.concatenate([x[lo:hi], ctx[i * CTXL:(i + 1) * CTXL], x[lo - 1:lo] if i > 0 else zero, x[hi:hi + 1] if i < NCORES - 1 else zero], 0)
        cos, sin = _rope_tables(lo, TL)
        hm = np.zeros((128, 2), np.float32)
        hm[:, 0] = 1.0 if i > 0 else 0.0
        hm[:, 1] = 1.0 if i < NCORES - 1 else 0.0
        m = dict(shared)
        m.update(xin=np.ascontiguousarray(xin), rope_cos=cos, rope_sin=sin, hmask=hm)
        maps.append(m)
    return maps


NBLK = NKEY // 128
NFFT = 2 * SEQ
MAGIC = 12582912.0
TWO_PI = 2.0 * math.pi


def build_B():
    C = Ctx()
    P = C.P
    common_consts(C)
    attention_B(C)
    hyena_B(C)
    return C


def attn_head_qt(C, qt, knh, knhb, vh, vhb, qn, qnb, qr, qrb, oh, ohb, krs, krsb, pTs, accd, accdb, accp, accpb, rs, rsb):
    P = C.P
    q0 = qt * 512
    po, pob = C.ps[4 + (qt % 2)], C.psb[4 + (qt % 2)]
    pend = {}

    def score(blk):
        pi = C.pick("sps", [0, 1, 2, 3])
        ps, psb = C.ps[pi], C.psb[pi]
        P.op(PE, lambda e: e.matmul(ps[:, :], knh[:, blk * 128:(blk + 1) * 128], qn[:, q0:q0 + 512], start=True, stop=False), reads=[knhb, qnb], writes=[psb])
        P.op(PE, lambda e: e.matmul(ps[:, :], krs[:, blk * 128:(blk + 1) * 128], qr[:, q0:q0 + 512], start=False, stop=True), reads=[krsb, qrb], writes=[psb])
        pT, pTb = C.pick("pT", pTs)
        P.op(ACT, lambda e: e.activation(out=pT[:], in_=ps[:, :], func=AF.Exp, scale=ATTN_SCALE), reads=[psb], writes=[pTb])
        pend[blk] = (pT, pTb)

    score(0)
    score(1)
    for blk in range(NBLK):
        if blk + 2 < NBLK:
            score(blk + 2)
        pT, pTb = pend.pop(blk)
        P.op(PE, lambda e, pT=pT, blk=blk: e.matmul(po[:, :], vh[:, blk, :], pT[:], start=(blk == 0), stop=(blk == NBLK - 1)), reads=[vhb, pTb], writes=[pob])
        if blk % 3 == 2:
            eng, acc, accb, first = POOL, accp, accpb, blk == 2
        else:
            eng, acc, accb, first = DVE, accd, accdb, blk == 0
        if first:
            P.op(eng, lambda e, acc=acc, pT=pT: e.tensor_copy(out=acc[:], in_=pT[:]), reads=[pTb], writes=[accb])
        else:
            P.op(eng, lambda e, acc=acc, pT=pT: e.tensor_tensor(out=acc[:], in0=acc[:], in1=pT[:], op=ALU.add), reads=[pTb, accb], writes=[accb])
    P.op(DVE, lambda e: e.tensor_tensor(out=accd[:], in0=accd[:], in1=accp[:], op=ALU.add), reads=[accdb, accpb], writes=[accdb])
    psum_s, psum_sb = C.ps[6 + (qt % 2)], C.psb[6 + (qt % 2)]
    P.op(PE, lambda e, psum_s=psum_s: e.matmul(psum_s[:, :], C.ones[:], accd[:], start=True, stop=True), reads=[C.onesb, accdb], writes=[psum_sb])
    P.op(DVE, lambda e, psum_s=psum_s: e.reciprocal(out=rs[:], in_=psum_s[:, :]), reads=[psum_sb], writes=[rsb])
    P.op(DVE, lambda e, po=po, oh=oh, q0=q0: e.tensor_tensor(out=oh[:, q0:q0 + 512], in0=po[:, :], in1=rs[:], op=ALU.mult), reads=[pob, rsb], writes=[ohb])


def attention_B(C):
    P = C.P
    q, qb = C.din("q", [16, 192, TL], BF16)
    kn, knb = C.din("kn_all", [16, 128, NKEY], BF16)
    kr, krb = C.din("kr_all", [64, NKEY], BF16)
    v, vb = C.din("v_all", [16, 128, NBLK * 128], BF16)
    o_attn, o_attnb = C.dout("o_attn", [128, 16, TL], BF16)
    C.push()
    krs, krsb = C.sb([64, NKEY], BF16, "krs")
    P.dma(SP, lambda e: e.dma_start(out=krs[:], in_=kr[:, :]), reads=[krb], writes=[krsb])
    kns = [C.sb([128, NKEY], BF16, "kns") for _ in range(2)]
    vs = [C.sb([128, NBLK, 128], BF16, "vs") for _ in range(2)]
    qns = [C.sb([128, TL], BF16, "qn") for _ in range(2)]
    qrs = [C.sb([64, TL], BF16, "qr") for _ in range(2)]
    pTs = [C.sb([128, 512], BF16, "pT") for _ in range(4)]
    accd, accdb = C.sb([128, 512], F32, "accd")
    accp, accpb = C.sb([128, 512], F32, "accp")
    rs, rsb = C.sb([128, 512], F32, "rs")
    ohs = [C.sb([128, TL], BF16, "oh") for _ in range(2)]
    for h in range(16):
        knh, knhb = C.pick("kns", kns)
        vh, vhb = C.pick("vs", vs)
        qn, qnb = C.pick("qn", qns)
        qr, qrb = C.pick("qr", qrs)
        oh, ohb = C.pick("oh", ohs)
        P.dma(SP, lambda e, knh=knh, h=h: e.dma_start(out=knh[:], in_=kn[h, :, :]), reads=[knb], writes=[knhb])
        P.dma(SP, lambda e, vh=vh, h=h: e.dma_start(out=vh[:].rearrange("p a b -> p (a b)"), in_=v[h, :, :]), reads=[vb], writes=[vhb])
        P.dma(SP, lambda e, qn=qn, h=h: e.dma_start(out=qn[:], in_=q[h, 0:128, :]), reads=[qb], writes=[qnb])
        P.dma(SP, lambda e, qr=qr, h=h: e.dma_start(out=qr[:], in_=q[h, 128:192, :]), reads=[qb], writes=[qrb])
        for qt in range(2):
            attn_head_qt(C, qt, knh, knhb, vh, vhb, qn, qnb, qr, qrb, oh, ohb, krs, krsb, pTs, accd, accdb, accp, accpb, rs, rsb)
        P.dma(SP, lambda e, oh=oh, h=h: e.dma_start(out=o_attn[:, h, :], in_=oh[:]), reads=[ohb], writes=[o_attnb])
    C.pop()


def hyena_B(C):
    P = C.P
    NCH = 128
    hyin, hyinb = C.din("hyin", [3, NCH, SEQ])
    zT_d, zT_db = C.din("zT", [33, SEQ])
    o_yhy, o_yhyb = C.dout("o_yhy", [64, NCH, 128], BF16)
    Hs = C.nc.dram_tensor("Hs", [2, 128, NCH, 256], F32)
    Hsb = [C.P.bufs(NCH // 4, "Hs%d" % o) for o in range(2)]
    C.push()
    cs = load_consts(C, [("hy_w1", [33, 64]), ("hy_w2", [64, 64]), ("w3c", [64, 256]), ("fb1", [64, 3]), ("fb2", [64, 3]),
                         ("distT", [64, 128]), ("ndecay", [64, NCH]), ("skipb", [64, 2, NCH]),
                         ("Tr", [128, 128]), ("Ti", [128, 128]), ("nTi", [128, 128])])
    fcs = {}
    for name, shape in [("FrFi", [128, 256]), ("FrnFi", [128, 256]), ("FiFr", [128, 256]), ("nFi", [128, 128]), ("Fr64", [128, 64]), ("Fi64", [128, 64])]:
        d, db = C.din(name, shape, BF16)
        t, tb = C.sb(shape, BF16, name)
        P.dma(SP, lambda e, t=t, d=d: e.dma_start(out=t[:], in_=d[:, :]), reads=[db], writes=[tb])
        fcs[name] = (t, tb)
    FrFi, FrFib = fcs["FrFi"]
    Tr, Trb = cs["Tr"]
    Ti, Tib = cs["Ti"]
    nTi, nTib = cs["nTi"]
    tw_f = (Tr, Trb, Ti, Tib, nTi, nTib)
    tw_i = (Tr, Trb, nTi, nTib, Ti, Tib)
    mts = [C.sb([128, 512], F32, "mt") for _ in range(4)]

    def cmul(ps, psb, n_c, tab, outT, outTb, c0):
        tre, treb, tim, timb, ntim, ntimb = tab
        pv = ps[:, :n_c * 256].rearrange("p (c r k) -> p c r k", c=n_c, r=2)
        ar, ai = pv[:, :, 0, :], pv[:, :, 1, :]
        bre = tre[:].unsqueeze(1).to_broadcast([128, n_c, 128]) if len(tre.shape) == 2 else tre
        bim = tim[:].unsqueeze(1).to_broadcast([128, n_c, 128]) if len(tim.shape) == 2 else tim
        bnim = ntim[:].unsqueeze(1).to_broadcast([128, n_c, 128]) if len(ntim.shape) == 2 else ntim
        m = [C.pick("mt", mts) for _ in range(4)]
        mv = [mm[0][:, :n_c * 128].rearrange("p (c k) -> p c k", c=n_c) for mm in m]
        P.op(DVE, lambda e: e.tensor_tensor(out=mv[0], in0=ar, in1=bre, op=ALU.mult), reads=[psb, treb], writes=[m[0][1]])
        P.op(DVE, lambda e: e.tensor_tensor(out=mv[1], in0=ai, in1=bnim, op=ALU.mult), reads=[psb, ntimb], writes=[m[1][1]])
        P.op(DVE, lambda e: e.tensor_tensor(out=mv[2], in0=ar, in1=bim, op=ALU.mult), reads=[psb, timb], writes=[m[2][1]])
        P.op(DVE, lambda e: e.tensor_tensor(out=mv[3], in0=ai, in1=bre, op=ALU.mult), reads=[psb, treb], writes=[m[3][1]])
        P.op(POOL, lambda e: e.tensor_tensor(out=outT[:, c0:c0 + n_c, 0, :], in0=mv[0], in1=mv[1], op=ALU.add), reads=[m[0][1], m[1][1]], writes=[outTb])
        P.op(POOL, lambda e: e.tensor_tensor(out=outT[:, c0:c0 + n_c, 1, :], in0=mv[2], in1=mv[3], op=ALU.add), reads=[m[2][1], m[3][1]], writes=[outTb])

    def fwd_fft(ub, ubb, n_c, Bs, Bsb, evac, cbase=0):
        for c in range(0, n_c, 2):
            pi_ = C.pick("fa", [0, 1, 2, 3])
            ps, psb = C.ps[pi_], C.psb[pi_]
            for k in range(2):
                P.op(PE, lambda e, ps=ps, c=c, k=k: e.matmul(ps[:, k * 256:(k + 1) * 256], ub[:, cbase + c + k, :], FrFi[0:64, :], start=True, stop=True), reads=[ubb, FrFib], writes=[psb])
            cmul(ps, psb, 2, tw_f, Bs, Bsb[c // 4], c)
        Fr, Frb = FrFi[:, 0:128], FrFib
        Fi = FrFi[:, 128:256]
        nFi, nFib = fcs["nFi"]
        for c in range(0, n_c, 4):
            pj = C.pick("fb", [4, 6])
            pr, prb, pim, pimb = C.ps[pj], C.psb[pj], C.ps[pj + 1], C.psb[pj + 1]
            br = Bs[:, c:c + 4, 0, :]
            bi = Bs[:, c:c + 4, 1, :]
            P.op(PE, lambda e, pr=pr, br=br: e.matmul(pr[:, :], Fr, br, start=True, stop=False), reads=[Frb, Bsb[c // 4]], writes=[prb])
            P.op(PE, lambda e, pr=pr, bi=bi: e.matmul(pr[:, :], nFi[:], bi, start=False, stop=True), reads=[nFib, Bsb[c // 4]], writes=[prb])
            P.op(PE, lambda e, pim=pim, br=br: e.matmul(pim[:, :], Fi, br, start=True, stop=False), reads=[Frb, Bsb[c // 4]], writes=[pimb])
            P.op(PE, lambda e, pim=pim, bi=bi: e.matmul(pim[:, :], Fr, bi, start=False, stop=True), reads=[Frb, Bsb[c // 4]], writes=[pimb])
            evac(pr, prb, pim, pimb, cbase + c)

    C.push()
    h2T, h2Tb = C.sb([64, SEQ], F32, "h2T")
    C.push()
    zT, zTb = C.sb([33, SEQ], F32, "zT")
    P.dma(SP, lambda e: e.dma_start(out=zT[:], in_=zT_d[:, :]), reads=[zT_db], writes=[zTb])
    h1T, h1Tb = C.sb([64, SEQ], F32, "h1T")
    args = [C.sb([64, 512], F32, "arg") for _ in range(2)]
    ks = [C.sb([64, 512], F32, "kk") for _ in range(2)]

    def sin_layer(w, wb, fb, fbb, src, srcb, dst, dstb, kdim):
        for t in range(SEQ // 512):
            pi_ = C.pick("fa", [0, 1, 2, 3])
            ps, psb = C.ps[pi_], C.psb[pi_]
            P.op(PE, lambda e, ps=ps, t=t: e.matmul(ps[:64, :], w[:kdim, :], src[:kdim, t * 512:(t + 1) * 512], start=True, stop=True), reads=[wb, srcb], writes=[psb])
            a, ab = C.pick("arg", args)
            k, kb = C.pick("kk", ks)
            P.op(ACT, lambda e, ps=ps, a=a: e.activation(out=a[:], in_=ps[:64, :], func=AF.Identity, scale=fb[:, 0:1], bias=fb[:, 2:3]), reads=[psb, fbb], writes=[ab])
            P.op(DVE, lambda e, a=a, k=k: e.tensor_scalar(out=k[:], in0=a[:], scalar1=1.0 / TWO_PI, scalar2=MAGIC, op0=ALU.mult, op1=ALU.add), reads=[ab], writes=[kb])
            P.op(DVE, lambda e, k=k: e.tensor_scalar(out=k[:], in0=k[:], scalar1=-MAGIC, scalar2=-TWO_PI, op0=ALU.add, op1=ALU.mult), reads=[kb], writes=[kb])
            P.op(DVE, lambda e, a=a, k=k: e.tensor_tensor(out=a[:], in0=a[:], in1=k[:], op=ALU.add), reads=[ab, kb], writes=[ab])
            P.op(DVE, lambda e, a=a: e.tensor_scalar(out=a[:], in0=a[:], scalar1=3.14159, scalar2=-3.14159, op0=ALU.min, op1=ALU.max), reads=[ab], writes=[ab])
            P.op(ACT, lambda e, a=a, t=t: e.activation(out=dst[:, t * 512:(t + 1) * 512], in_=a[:], func=AF.Sin), reads=[ab], writes=[dstb])

    for nm in ("fb1", "fb2"):
        fbt, fbtb = cs[nm]
        P.op(DVE, lambda e, fbt=fbt: e.tensor_tensor(out=fbt[:, 2:3], in0=fbt[:, 0:1], in1=fbt[:, 1:2], op=ALU.mult), reads=[fbtb], writes=[fbtb])
    sin_layer(cs["hy_w1"][0], cs["hy_w1"][1], cs["fb1"][0], cs["fb1"][1], zT, zTb, h1T, h1Tb, 33)
    sin_layer(cs["hy_w2"][0], cs["hy_w2"][1], cs["fb2"][0], cs["fb2"][1], h1T, h1Tb, h2T, h2Tb, 64)
    C.pop()
    w3c, w3cb = cs["w3c"]
    distT, distTb = cs["distT"]
    ndecay, ndecayb = cs["ndecay"]
    h2v = h2T[:, :].rearrange("p (a b) -> p b a", b=128)
    filtb = [C.sb([64, NCH, 128], BF16, "filtb%d" % o) for o in range(2)]
    fbufs = [C.P.bufs(NCH // 4, "filtb") for o in range(2)]
    sqacc = [C.sb([64, NCH], F32, "sqacc%d" % o) for o in range(2)]
    for o in range(2):
        P.op(POOL, lambda e, o=o: e.memset(sqacc[o][0][:], 0.0), writes=[sqacc[o][1]])
    wins = [C.sb([64, NCH], F32, "win") for _ in range(2)]
    tfs = [C.sb([64, NCH], F32, "tf") for _ in range(3)]
    for s2 in range(128):
        wn, wnb = C.pick("win", wins)
        P.op(ACT, lambda e, wn=wn, s2=s2: e.activation(out=wn[:], in_=ndecay[:], func=AF.Exp, scale=distT[:, s2:s2 + 1]), reads=[ndecayb, distTb], writes=[wnb])
        for o in range(2):
            pi_ = C.pick("fa", [0, 1, 2, 3])
            ps, psb = C.ps[pi_], C.psb[pi_]
            P.op(PE, lambda e, ps=ps, s2=s2, o=o: e.matmul(ps[:64, :NCH], h2v[:, s2, :], w3c[:, o * NCH:(o + 1) * NCH], start=True, stop=True), reads=[h2Tb, w3cb], writes=[psb])
            tf, tfb = C.pick("tf", tfs)
            P.op(DVE, lambda e, ps=ps, wn=wn, tf=tf: e.scalar_tensor_tensor(out=tf[:], in0=wn[:], scalar=0.05, in1=ps[:64, :NCH], op0=ALU.add, op1=ALU.mult), reads=[wnb, psb], writes=[tfb])
            P.op(ACT, lambda e, tf=tf, o=o, s2=s2: e.activation(out=filtb[o][0][:, :, s2], in_=tf[:], func=AF.Copy), reads=[tfb], writes=fbufs[o])
            sq, sqb = C.pick("tf", tfs)
            P.op(ACT, lambda e, tf=tf, sq=sq: e.activation(out=sq[:], in_=tf[:], func=AF.Square), reads=[tfb], writes=[sqb])
            P.op(POOL, lambda e, sq=sq, o=o: e.tensor_tensor(out=sqacc[o][0][:], in0=sqacc[o][0][:], in1=sq[:], op=ALU.add), reads=[sqb, sqacc[o][1]], writes=[sqacc[o][1]])
    normrow = [C.sb([128, NCH], F32, "normrow%d" % o) for o in range(2)]
    for o in range(2):
        pi_ = C.pick("fa", [0, 1, 2, 3])
        ps, psb = C.ps[pi_], C.psb[pi_]
        P.op(PE, lambda e, ps=ps, o=o: e.matmul(ps[:, :NCH], C.ones[0:64, :], sqacc[o][0][:], start=True, stop=True), reads=[C.onesb, sqacc[o][1]], writes=[psb])
        nr, nrb = normrow[o]
        P.op(ACT, lambda e, ps=ps, nr=nr: e.activation(out=nr[:], in_=ps[:, :NCH], func=AF.Sqrt, bias=C.epsb[0][:, 0:1]), reads=[psb, C.epsb[1]], writes=[nrb])
        P.op(DVE, lambda e, nr=nr: e.reciprocal(out=nr[:], in_=nr[:]), reads=[nrb], writes=[nrb])
        P.op(DVE, lambda e, nr=nr: e.tensor_scalar(out=nr[:], in0=nr[:], scalar1=1.0 / NFFT, scalar2=None, op0=ALU.mult), reads=[nrb], writes=[nrb])
    Bf, _ = C.sb([128, 16, 2, 128], BF16, "Bf")
    Bfb = C.P.bufs(4, "Bf")
    hst = [C.sb([128, 4, 256], F32, "hst") for _ in range(2)]
    for o in range(2):
        nr, nrb = normrow[o]

        def ev_H(pr, prb, pim, pimb, c, o=o, nr=nr, nrb=nrb):
            ht, htb = C.pick("hst", hst)
            nb = nr[:, c:c + 4].unsqueeze(2).to_broadcast([128, 4, 128])
            P.op(DVE, lambda e: e.tensor_tensor(out=ht[:, :, 0:128], in0=pr[:, :].rearrange("p (c k) -> p c k", c=4), in1=nb, op=ALU.mult), reads=[prb, nrb], writes=[htb])
            P.op(DVE, lambda e: e.tensor_tensor(out=ht[:, :, 128:256], in0=pim[:, :].rearrange("p (c k) -> p c k", c=4), in1=nb, op=ALU.mult), reads=[pimb, nrb], writes=[htb])
            P.dma(SP, lambda e: e.dma_start(out=Hs[o, :, c:c + 4, :], in_=ht[:]), reads=[htb], writes=[Hsb[o][c // 4]])

        fwd_fft_src = filtb[o][0]
        for g in range(NCH // 16):
            fwd_fft(fwd_fft_src, fbufs[o][0], 16, Bf, Bfb, ev_H, cbase=g * 16)
    C.pop()

    CG = 16
    C.push()
    skipb, skipbb = cs["skipb"]
    FrnFi, FrnFib = fcs["FrnFi"]
    FiFr, FiFrb = fcs["FiFr"]
    Fr64, Fr64b = fcs["Fr64"]
    Fi64, Fi64b = fcs["Fi64"]
    vt, vtb = C.sb([64, CG, 128], F32, "vt")
    x1t, x1tb = C.sb([64, CG, 128], F32, "x1t")
    x2t, x2tb = C.sb([64, CG, 128], F32, "x2t")
    zt, ztb = C.sb([64, CG, 128], F32, "zt")
    ub, ubb = C.sb([64, CG, 128], BF16, "ub")
    zb, zbb = C.sb([64, CG, 128], BF16, "zb")
    yo, yob = C.sb([64, CG, 128], BF16, "yo")
    Bs, _ = C.sb([128, CG, 2, 128], BF16, "Bs")
    Bsb = C.P.bufs(CG // 4, "Bs")
    Ys, _ = C.sb([128, CG, 2, 128], BF16, "Ys")
    Ysb = C.P.bufs(CG // 4, "Ys")
    Cs, _ = C.sb([128, CG, 2, 128], BF16, "Cs")
    Csb = C.P.bufs(CG // 4, "Cs")
    Ht = [C.sb([128, CG, 256], F32, "Ht%d" % o) for o in range(2)]
    tt1, tt1b = C.sb([64, 512], F32, "tt1")

    def conv(src_bf, src_bfb, Hto, Htob, fin):
        def ev_Y(pr, prb, pim, pimb, c):
            hv = Hto[:, c:c + 4, :].rearrange("p c (r k) -> p c r k", r=2)
            hr, hi = hv[:, :, 0, :], hv[:, :, 1, :]
            xr = pr[:, :].rearrange("p (c k) -> p c k", c=4)
            xi = pim[:, :].rearrange("p (c k) -> p c k", c=4)
            m = [C.pick("mt", mts) for _ in range(4)]
            mv = [mm[0][:, :].rearrange("p (c k) -> p c k", c=4) for mm in m]
            P.op(DVE, lambda e: e.tensor_tensor(out=mv[0], in0=xr, in1=hr, op=ALU.mult), reads=[prb, Htob], writes=[m[0][1]])
            P.op(DVE, lambda e: e.tensor_tensor(out=mv[1], in0=xi, in1=hi, op=ALU.mult), reads=[pimb, Htob], writes=[m[1][1]])
            P.op(DVE, lambda e: e.tensor_tensor(out=mv[2], in0=xr, in1=hi, op=ALU.mult), reads=[prb, Htob], writes=[m[2][1]])
            P.op(DVE, lambda e: e.tensor_tensor(out=mv[3], in0=xi, in1=hr, op=ALU.mult), reads=[pimb, Htob], writes=[m[3][1]])
            P.op(POOL, lambda e: e.tensor_tensor(out=Ys[:, c:c + 4, 0, :], in0=mv[0], in1=mv[1], op=ALU.subtract), reads=[m[0][1], m[1][1]], writes=[Ysb[c // 4]])
            P.op(POOL, lambda e: e.tensor_tensor(out=Ys[:, c:c + 4, 1, :], in0=mv[2], in1=mv[3], op=ALU.add), reads=[m[2][1], m[3][1]], writes=[Ysb[c // 4]])

        fwd_fft(src_bf, src_bfb, CG, Bs, Bsb, ev_Y)
        for c in range(0, CG, 2):
            pi_ = C.pick("fa", [0, 1, 2, 3])
            ps, psb = C.ps[pi_], C.psb[pi_]
            for k in range(2):
                P.op(PE, lambda e, ps=ps, c=c, k=k: e.matmul(ps[:, k * 256:(k + 1) * 256], Ys[:, c + k, 0, :], FrnFi[:], start=True, stop=False), reads=[Ysb[c // 4], FrnFib], writes=[psb])
                P.op(PE, lambda e, ps=ps, c=c, k=k: e.matmul(ps[:, k * 256:(k + 1) * 256], Ys[:, c + k, 1, :], FiFr[:], start=False, stop=True), reads=[Ysb[c // 4], FiFrb], writes=[psb])
            cmul(ps, psb, 2, tw_i, Cs, Csb[c // 4], c)
        for c in range(0, CG, 4):
            pj = C.pick("fb", [4, 6])
            py, pyb = C.ps[pj], C.psb[pj]
            cr = Cs[:, c:c + 4, 0, :]
            ci = Cs[:, c:c + 4, 1, :]
            P.op(PE, lambda e, py=py, cr=cr: e.matmul(py[:64, :], Fr64[:], cr, start=True, stop=False), reads=[Fr64b, Csb[c // 4]], writes=[pyb])
            P.op(PE, lambda e, py=py, ci=ci: e.matmul(py[:64, :], Fi64[:], ci, start=False, stop=True), reads=[Fi64b, Csb[c // 4]], writes=[pyb])
            fin(py, pyb, c)

    for g in range(NCH // CG):
        c0 = g * CG
        for (t, tb, grp) in [(vt, vtb, 0), (x1t, x1tb, 1), (x2t, x2tb, 2)]:
            P.dma(SP, lambda e, t=t, grp=grp, c0=c0: e.dma_start(out=t[:], in_=dap(hyin, grp * NCH * SEQ + c0 * SEQ, [[128, 64], [SEQ, CG], [1, 128]])), reads=[hyinb], writes=[tb])
        for o in range(2):
            P.dma(SP, lambda e, o=o, c0=c0: e.dma_start(out=Ht[o][0][:], in_=Hs[o, :, c0:c0 + CG, :]), reads=Hsb[o][c0 // 4:(c0 + CG) // 4], writes=[Ht[o][1]])
        P.op(ACT, lambda e: e.activation(out=ub[:], in_=vt[:], func=AF.Copy), reads=[vtb], writes=[ubb])

        def fin1(py, pyb, c, c0=c0):
            yv = py[:64, :].rearrange("p (c k) -> p c k", c=4)
            tv = tt1[:, :].rearrange("p (c k) -> p c k", c=4)
            sk = skipb[:, 0, c0 + c:c0 + c + 4].unsqueeze(2).to_broadcast([64, 4, 128])
            P.op(DVE, lambda e: e.tensor_tensor(out=tv, in0=vt[:, c:c + 4, :], in1=sk, op=ALU.mult), reads=[vtb, skipbb], writes=[tt1b])
            P.op(DVE, lambda e: e.tensor_tensor(out=tv, in0=tv, in1=yv, op=ALU.add), reads=[tt1b, pyb], writes=[tt1b])
            P.op(DVE, lambda e: e.tensor_tensor(out=zt[:, c:c + 4, :], in0=tv, in1=x1t[:, c:c + 4, :], op=ALU.mult), reads=[tt1b, x1tb], writes=[ztb])

        conv(ub, ubb, Ht[0][0], Ht[0][1], fin1)
        P.op(ACT, lambda e: e.activation(out=zb[:], in_=zt[:], func=AF.Copy), reads=[ztb], writes=[zbb])

        def fin2(py, pyb, c, c0=c0):
            yv = py[:64, :].rearrange("p (c k) -> p c k", c=4)
            tv = tt1[:, :].rearrange("p (c k) -> p c k", c=4)
            sk = skipb[:, 1, c0 + c:c0 + c + 4].unsqueeze(2).to_broadcast([64, 4, 128])
            P.op(DVE, lambda e: e.tensor_tensor(out=tv, in0=zt[:, c:c + 4, :], in1=sk, op=ALU.mult), reads=[ztb, skipbb], writes=[tt1b])
            P.op(DVE, lambda e: e.tensor_tensor(out=tv, in0=tv, in1=yv, op=ALU.add), reads=[tt1b, pyb], writes=[tt1b])
            P.op(DVE, lambda e: e.tensor_tensor(out=yo[:, c:c + 4, :], in0=tv, in1=x2t[:, c:c + 4, :], op=ALU.mult), reads=[tt1b, x2tb], writes=[yob])

        conv(zb, zbb, Ht[1][0], Ht[1][1], fin2)
        P.dma(SP, lambda e, c0=c0: e.dma_start(out=o_yhy[:, c0:c0 + CG, :], in_=yo[:]), reads=[yob], writes=[o_yhyb])
    C.pop()
    C.pop()


def _bf(a):
    return np.ascontiguousarray(np.asarray(a).astype(ml_dtypes.bfloat16))


def hyena_consts():
    n = np.arange(128)
    ang = 2.0 * np.pi * np.outer(n, n) / 128.0
    Fr = np.cos(ang)
    Fi = -np.sin(ang)
    angt = 2.0 * np.pi * np.outer(n, n) / NFFT
    Tr = np.cos(angt).astype(np.float32)
    Ti = (-np.sin(angt)).astype(np.float32)
    L = SEQ
    pos = np.arange(L, dtype=np.float32)[:, None]
    t01 = pos / np.float32(L - 1)
    bands = np.linspace(1e-4, 15, 16, dtype=np.float32)[None, :]
    a = bands * np.float32(2.0 * math.pi / L) * pos
    z = np.concatenate([t01, np.cos(a), -np.sin(a)], axis=-1).astype(np.float32)
    dist = (np.abs(pos[:, 0] - (L // 2)) / np.float32(L / 2.0)).astype(np.float32)
    return dict(
        FrFi=_bf(np.concatenate([Fr, Fi], 1)), FrnFi=_bf(np.concatenate([Fr, -Fi], 1)), FiFr=_bf(np.concatenate([Fi, Fr], 1)),
        nFi=_bf(-Fi), Fr64=_bf(Fr[:, 32:96]), Fi64=_bf(Fi[:, 32:96]),
        Tr=Tr, Ti=Ti, nTi=np.ascontiguousarray(-Ti),
        zT=np.ascontiguousarray(z.T), distT=np.ascontiguousarray(dist.reshape(64, 128)),
    )


def prep_B(I, resA):
    hc = hyena_consts()
    kn = np.concatenate([np.asarray(r["o_kn"]) for r in resA], axis=2)
    kr = np.concatenate([np.asarray(r["o_kr"]) for r in resA], axis=1)
    v = np.concatenate([np.asarray(r["o_v"]) for r in resA], axis=0)
    kn = np.ascontiguousarray(kn.reshape(16, 128, 128, NBLK).transpose(0, 1, 3, 2).reshape(16, 128, NKEY))
    kr = np.ascontiguousarray(kr.reshape(64, 128, NBLK).transpose(0, 2, 1).reshape(64, NKEY))
    v = np.ascontiguousarray(v.reshape(128, NBLK, 16, 128).transpose(2, 0, 1, 3).reshape(16, 128, NBLK * 128))
    hy = np.concatenate([np.asarray(r["o_hy"]) for r in resA], axis=2)
    decay = np.abs(np.linspace(math.log(1e-2) / 1.5, math.log(1e-2) / 0.3, 1024, dtype=np.float32))
    w3 = I["hy_w3"][0]
    skip = I["hy_skip"][0]
    fb1 = np.zeros((64, 3), np.float32)
    fb1[:, 0] = I["hy_freq"][0]
    fb1[:, 1] = I["hy_b1"][0]
    fb2 = fb1.copy()
    fb2[:, 1] = I["hy_b2"][0]
    maps = []
    for i in range(NCORES):
        ch = slice(i * 128, (i + 1) * 128)
        m = dict(hc)
        m.update(
            ident=np.eye(128, dtype=np.float32),
            q=np.asarray(resA[i]["o_q"]), kn_all=kn, kr_all=kr, v_all=v,
            hyin=np.ascontiguousarray(np.stack([hy[g * 8 + i] for g in range(3)], 0)),
            hy_w1=np.ascontiguousarray(I["hy_w1"][0]), hy_w2=np.ascontiguousarray(I["hy_w2"][0]),
            w3c=np.ascontiguousarray(np.concatenate([w3[:, ch], w3[:, 1024 + i * 128:1024 + (i + 1) * 128]], 1)),
            fb1=fb1, fb2=fb2,
            ndecay=np.ascontiguousarray(np.broadcast_to(-decay[ch][None, :], (64, 128))),
            skipb=np.ascontiguousarray(np.broadcast_to(skip[:, ch][None], (64, 2, 128))),
        )
        maps.append(m)
    return maps


def build_C():
    C = Ctx()
    P = C.P
    common_consts(C)
    cs = load_consts(C, [("modT", [128, 9 * KC, 2]), ("g2T", [128, KC]), ("gfinT", [128, KC])])
    modT, modb = cs["modT"]
    xT_d, xT_db = C.din("xT", [128, KC, TL])
    hxT_d, hxT_db = C.din("hxT", [128, KC, TL], BF16)
    attn_d, attn_db = C.din("attnT", [128, 16, TL], BF16)
    yhy_d, yhy_db = C.din("yhyT", [128, 8, TL], BF16)
    w_in, w_inb = C.din("w_in", [D, 8256])
    w_ao, w_aob = C.din("w_attn_o", [D, D])
    w_ho, w_hob = C.din("w_hy_o", [1024, D])
    w_out, w_outb = C.din("w_out", [D, D])
    w13, w13b = C.din("w13", [D, 2 * DFF])
    w2, w2b = C.din("w2", [DFF, D])
    out_d, out_db = C.dout("out", [TL, D])
    slots = [C.sb([128, KC, 128], BF16, "wslot") for _ in range(8)]
    tmp_pool = [C.sb([128, NCOL], F32, "tmp") for _ in range(2)]
    LT = [(0, 512), (512, 512)]

    def wload(wd, wdb, row0, nk, col0, ncols_w):
        w, wb = C.pick("slot", slots)
        P.dma(POOL, lambda e: e.dma_start(out=w[:, :nk, :], in_=dap(wd, row0 * ncols_w + col0, [[ncols_w, 128], [128 * ncols_w, nk], [1, 128]])), reads=[wdb], writes=[wb])
        return w, wb

    def acc(pi, w, wb, nk, src, srcb, s, n):
        ps, psb = C.ps[pi], C.psb[pi]
        for k in range(nk):
            P.op(PE, lambda e, k=k: e.matmul(ps[:, :n], w[:, k, :], src[:, k, s:s + n], start=(k == 0), stop=(k == nk - 1)), reads=[wb, srcb], writes=[psb])
        return ps, psb

    C.push()
    yT, yTb = C.sb([128, KC, TL], BF16, "yT")
    C.push()
    hT, hTb = C.sb([128, KC, TL], BF16, "hT")
    aT, aTb = C.sb([128, 16, TL], BF16, "aT")
    yh, yhb = C.sb([128, 8, TL], BF16, "yh")
    P.dma(SP, lambda e: e.dma_start(out=hT[:], in_=hxT_d[:, :, :]), reads=[hxT_db], writes=[hTb])
    P.dma(SP, lambda e: e.dma_start(out=aT[:], in_=attn_d[:, :, :]), reads=[attn_db], writes=[aTb])
    P.dma(SP, lambda e: e.dma_start(out=yh[:], in_=yhy_d[:, :, :]), reads=[yhy_db], writes=[yhb])
    sg = [C.sb([128, 512], F32, "sg") for _ in range(4)]
    for oc in range(KC):
        wa, wab = wload(w_ao, w_aob, 0, 16, oc * 128, D)
        wh, whb = wload(w_ho, w_hob, 0, 8, oc * 128, D)
        wg1, wg1b = wload(w_in, w_inb, 0, 16, 4160 + oc * 128, 8256)
        wg2, wg2b = wload(w_in, w_inb, 0, 16, 4160 + 2048 + oc * 128, 8256)
        for (s, n) in LT:
            base = C.pick("mps", [0, 4])
            pa, pab = acc(base, wa, wab, 16, aT, aTb, s, n)
            ph, phb = acc(base + 1, wh, whb, 8, yh, yhb, s, n)
            pg1, pg1b = acc(base + 2, wg1, wg1b, 16, hT, hTb, s, n)
            pg2, pg2b = acc(base + 3, wg2, wg2b, 16, hT, hTb, s, n)
            s1, s1b = C.pick("sg", sg)
            s2, s2b = C.pick("sg", sg)

            def emit(pa=pa, pab=pab, ph=ph, phb=phb, pg1=pg1, pg1b=pg1b, pg2=pg2, pg2b=pg2b, s1=s1, s1b=s1b, s2=s2, s2b=s2b, oc=oc, s=s, n=n):
                P.op(ACT, lambda e: e.activation(out=s1[:, :n], in_=pg1[:, :n], func=AF.Sigmoid), reads=[pg1b], writes=[s1b])
                P.op(ACT, lambda e: e.activation(out=s2[:, :n], in_=pg2[:, :n], func=AF.Sigmoid), reads=[pg2b], writes=[s2b])
                P.op(DVE, lambda e: e.tensor_tensor(out=s1[:, :n], in0=pa[:, :n], in1=s1[:, :n], op=ALU.mult), reads=[pab, s1b], writes=[s1b])
                P.op(DVE, lambda e: e.tensor_tensor(out=s2[:, :n], in0=ph[:, :n], in1=s2[:, :n], op=ALU.mult), reads=[phb, s2b], writes=[s2b])
                P.op(POOL, lambda e: e.tensor_tensor(out=yT[:, oc, s:s + n], in0=s1[:, :n], in1=s2[:, :n], op=ALU.add), reads=[s1b, s2b], writes=[yTb])
            emit()
    C.pop()
    C.push()
    xT, xTb = C.sb([128, KC, TL], F32, "xT")
    P.dma(SP, lambda e: e.dma_start(out=xT[:], in_=xT_d[:, :, :]), reads=[xT_db], writes=[xTb])
    for oc in range(KC):
        wo, wob = wload(w_out, w_outb, 0, 16, oc * 128, D)
        for (s, n) in LT:
            pi = C.pick("ops", [0, 1, 2, 3])
            po, pob = acc(pi, wo, wob, 16, yT, yTb, s, n)
            P.op(DVE, lambda e, po=po, oc=oc, s=s, n=n: e.scalar_tensor_tensor(out=xT[:, oc, s:s + n], in0=po[:, :n], scalar=modT[:, 5 * KC + oc, 0:1], in1=xT[:, oc, s:s + n], op0=ALU.mult, op1=ALU.add), reads=[pob, modb, xTb], writes=[xTb])
    P.barrier()
    hT2, hT2b = yT, yTb
    rms_modulate(C, xT, xTb, hT2, hT2b, TL, cs["g2T"], modT, modb, 6, 7, C.ones, C.onesb, tmp_pool)
    C.push()
    ffn(C, xT, xTb, hT2, hT2b, w13, w13b, w2, w2b, modT, modb, 8, slots[:4], TL, TL, TL, tmp_pool)
    C.pop()
    gfin, gfinb = cs["gfinT"]
    rstd, rstdb = C.sb([128, TL], F32, "rstdf")
    accs = [(C.ps[5 + i], C.psb[5 + i]) for i in range(2)]
    for kc in range(KC):
        sq, sqb = tmp_pool[kc % 2]
        P.op(ACT, lambda e, sq=sq, kc=kc: e.activation(out=sq[:, :TL], in_=xT[:, kc, :], func=AF.Square), reads=[xTb], writes=[sqb])
        for ti, (s, n) in enumerate(LT):
            ps, psb = accs[ti]
            P.op(PE, lambda e, ps=ps, sq=sq, s=s, n=n, kc=kc: e.matmul(ps[:, :n], C.ones[:], sq[:, s:s + n], start=(kc == 0), stop=(kc == KC - 1)), reads=[sqb, C.onesb], writes=[psb])
    for ti, (s, n) in enumerate(LT):
        ps, psb = accs[ti]
        P.op(ACT, lambda e, ps=ps, s=s, n=n: e.activation(out=rstd[:, s:s + n], in_=ps[:, :n], func=AF.Sqrt, scale=1.0 / D, bias=C.epsb[0][:, 0:1]), reads=[psb, C.epsb[1]], writes=[rstdb])
    P.op(DVE, lambda e: e.reciprocal(out=rstd[:], in_=rstd[:]), reads=[rstdb], writes=[rstdb])
    xkb = C.P.bufs(KC, "xk")
    for kc in range(KC):
        t, tb = tmp_pool[kc % 2]
        P.op(DVE, lambda e, t=t, kc=kc: e.tensor_tensor(out=t[:, :TL], in0=xT[:, kc, :], in1=rstd[:], op=ALU.mult), reads=[xTb, rstdb], writes=[tb])
        P.op(ACT, lambda e, t=t, kc=kc: e.activation(out=xT[:, kc, :], in_=t[:, :TL], func=AF.Identity, scale=gfin[:, kc:kc + 1]), reads=[tb, gfinb, xTb], writes=[xkb[kc]])
    otoks = [C.sb([128, D], F32, "otok") for _ in range(2)]
    for tb_ in range(TL // 128):
        ot, otb = C.pick("otok", otoks)
        for q4 in range(4):
            pi = C.pick("trps", [0, 1, 2, 3])
            pt, ptb = C.ps[pi], C.psb[pi]
            for k in range(4):
                kc = q4 * 4 + k
                P.op(PE, lambda e, pt=pt, kc=kc, k=k, tb_=tb_: e.transpose(pt[:, k * 128:(k + 1) * 128], xT[:, kc, tb_ * 128:(tb_ + 1) * 128], C.ident[:]), reads=[xkb[kc], C.identb], writes=[ptb])
            P.op(ACT, lambda e, pt=pt, ot=ot, q4=q4: e.activation(out=ot[:, q4 * 512:(q4 + 1) * 512], in_=pt[:, :], func=AF.Copy), reads=[ptb], writes=[otb])
        P.dma(SP, lambda e, ot=ot, tb_=tb_: e.dma_start(out=out_d[tb_ * 128:(tb_ + 1) * 128, :], in_=ot[:]), reads=[otb], writes=[out_db])
    C.pop()
    C.pop()
    return C


def prep_C(I, resA, resB):
    maps = []
    yall = [np.asarray(r["o_yhy"]) for r in resB]
    for i in range(NCORES):
        yh = np.stack([y[8 * i:8 * i + 8] for y in yall], 0)
        yh = np.ascontiguousarray(yh.transpose(2, 0, 1, 3).reshape(128, 8, TL))
        maps.append(dict(
            ident=np.eye(128, dtype=np.float32),
            modT=np.asarray(resA[i]["o_mod"]), g2T=_pl(I["g_ffn2"][0], KC), gfinT=_pl(I["g_final"], KC),
            xT=np.asarray(resA[i]["o_xT"]), hxT=np.asarray(resA[i]["o_hxT"]), attnT=np.asarray(resB[i]["o_attn"]), yhyT=yh,
            w_in=I["w_in"][0], w_attn_o=I["w_attn_o"][0], w_hy_o=I["w_hy_o"][0], w_out=I["w_out"][0],
            w13=I["w13_ffn2"][0], w2=I["w2_ffn2"][0],
        ))
    return maps


_PROGS = {}


def _prog(name, fn):
    if name not in _PROGS:
        _PROGS[name] = fn().finish()
    return _PROGS[name]


def kernel(**inputs):
    I = {k: np.asarray(v) for k, v in inputs.items()}
    cores = list(range(NCORES))
    resA = run_bass_kernel_spmd(_prog("A", build_A), prep_A(I), core_ids=cores).results
    resB = run_bass_kernel_spmd(_prog("B", build_B), prep_B(I, resA), core_ids=cores).results
    resC = run_bass_kernel_spmd(_prog("C", build_C), prep_C(I, resA, resB), core_ids=cores).results
    out = np.concatenate([np.asarray(r["out"]) for r in resC], axis=0)
    return out.reshape(1, SEQ, D).astype(np.float32)
```
